# Optimizing a Trainium2 kernel written in Bass

```python
import jax, jax.numpy as jnp
from jax import lax
import numpy as np

D_MODEL = 1024
BATCH = 32
SEQ = 2048
DEPTH = 1

RET_HEADS = 4
RET_DK = 128
RET_DV = 256
RET_CHUNK = 128
FOX_HEADS = 8
FOX_DH = 64
FOX_BLOCK = 128
PEER_HEADS = 8
PEER_NKEYS = 128
PEER_NEXP = PEER_NKEYS * PEER_NKEYS
PEER_DQ = 256
PEER_HALF = PEER_DQ // 2
PEER_TOPK = 16
PEER_TOK_BLOCK = 128
PLE_DIM = 256
LN_EPS = 1e-5
ROPE_BASE = 10000.0
ALPHA = (2.0 * DEPTH) ** 0.25
BETA = (8.0 * DEPTH) ** -0.25

RET_QK_W = RET_HEADS * RET_DK
RET_V_W = RET_HEADS * RET_DV
FOX_W = FOX_HEADS * FOX_DH
IN_SPLITS = (RET_QK_W, RET_QK_W, RET_V_W, RET_V_W, FOX_W, FOX_W, FOX_W, FOX_HEADS, D_MODEL, D_MODEL)
N_IN = sum(IN_SPLITS)

kernel_name = "hybrid_retention_fox_peer_deepnorm"


def layer_norm(x, g, b):
    xf = x.astype(jnp.float32)
    mu = jnp.mean(xf, axis=-1, keepdims=True)
    var = jnp.mean(jnp.square(xf - mu), axis=-1, keepdims=True)
    y = (xf - mu) * lax.rsqrt(var + LN_EPS)
    return (y * g.astype(jnp.float32) + b.astype(jnp.float32)).astype(x.dtype)


def group_norm_heads(y):
    mu = jnp.mean(y, axis=-1, keepdims=True)
    var = jnp.mean(jnp.square(y - mu), axis=-1, keepdims=True)
    return (y - mu) * lax.rsqrt(var + LN_EPS)


def rotary(x, pos):
    d = x.shape[-1]
    half = d // 2
    inv = ROPE_BASE ** (-jnp.arange(half, dtype=jnp.float32) / half)
    ang = pos.astype(jnp.float32)[:, None] * inv[None, :]
    cos = jnp.cos(ang)[None, :, None, :]
    sin = jnp.sin(ang)[None, :, None, :]
    xf = x.astype(jnp.float32)
    x1, x2 = xf[..., :half], xf[..., half:]
    return jnp.concatenate([x1 * cos - x2 * sin, x2 * cos + x1 * sin], axis=-1)


def retention(q, k, v):
    B, S, H, dk = q.shape
    dv = v.shape[-1]
    C = RET_CHUNK
    n = S // C
    log_g = jnp.log(1.0 - 2.0 ** (-5.0 - jnp.arange(H, dtype=jnp.float32)))
    idx = jnp.arange(C, dtype=jnp.float32)
    diff = idx[:, None] - idx[None, :]
    intra = jnp.where(diff >= 0, jnp.exp(log_g[:, None, None] * jnp.maximum(diff, 0.0)), 0.0)
    q_decay = jnp.exp(log_g[:, None] * (idx + 1.0))[..., None]
    k_decay = jnp.exp(log_g[:, None] * (C - 1.0 - idx))[..., None]
    chunk_decay = jnp.exp(log_g * C)[:, None, None]

    def to_chunks(t):
        return t.reshape(B, n, C, H, t.shape[-1]).transpose(1, 0, 3, 2, 4)

    qc, kc, vc = to_chunks(q), to_chunks(k), to_chunks(v)

    def step(R, inp):
        qi, ki, vi = inp
        s = jnp.einsum('bhid,bhjd->bhij', qi, ki) * intra
        inner = jnp.einsum('bhij,bhjv->bhiv', s, vi)
        cross = jnp.einsum('bhid,bhdv->bhiv', qi * q_decay, R)
        R_new = chunk_decay * R + jnp.einsum('bhjd,bhjv->bhdv', ki * k_decay, vi)
        return R_new, inner + cross

    R0 = jnp.zeros((B, H, dk, dv), jnp.float32)
    _, out = lax.scan(step, R0, (qc, kc, vc))
    return out.transpose(1, 0, 3, 2, 4).reshape(B, S, H, dv)


def forgetting_attention(q, k, v, log_f):
    B, S, H, d = q.shape
    nb = S // FOX_BLOCK
    c = jnp.cumsum(log_f, axis=1).transpose(0, 2, 1)
    qh = q.transpose(0, 2, 1, 3)
    kh = k.transpose(0, 2, 1, 3)
    vh = v.transpose(0, 2, 1, 3)
    qb = qh.reshape(B, H, nb, FOX_BLOCK, d).transpose(2, 0, 1, 3, 4)
    cb = c.reshape(B, H, nb, FOX_BLOCK).transpose(2, 0, 1, 3)
    starts = jnp.arange(nb, dtype=jnp.int32) * FOX_BLOCK
    kpos = jnp.arange(S, dtype=jnp.int32)
    scale = d ** -0.5

    def block(inp):
        qi, ci, start = inp
        qpos = start + jnp.arange(FOX_BLOCK, dtype=jnp.int32)
        logits = jnp.einsum('bhqd,bhkd->bhqk', qi, kh).astype(jnp.float32) * scale
        logits = logits + ci[..., None] - c[:, :, None, :]
        logits = jnp.where(kpos[None, :] <= qpos[:, None], logits, -1e30)
        probs = jax.nn.softmax(logits, axis=-1)
        return jnp.einsum('bhqk,bhkd->bhqd', probs.astype(vh.dtype), vh)

    out = lax.map(block, (qb, cb, starts))
    return out.transpose(1, 0, 3, 2, 4).reshape(B, S, H * d)


def peer(x, w_q, sub_keys, expert_u, expert_v):
    B, S, D = x.shape
    nt = (B * S) // PEER_TOK_BLOCK
    xt = x.reshape(nt, PEER_TOK_BLOCK, D)

    def block(xc):
        n = xc.shape[0]
        q = (xc @ w_q).reshape(n, PEER_HEADS, 2, PEER_HALF)
        sc = jnp.einsum('nhcd,hckd->nhck', q, sub_keys).astype(jnp.float32)
        s, idx = lax.top_k(sc, PEER_TOPK)
        cand = (s[:, :, 0, :, None] + s[:, :, 1, None, :]).reshape(n, PEER_HEADS, PEER_TOPK * PEER_TOPK)
        cid = (idx[:, :, 0, :, None] * PEER_NKEYS + idx[:, :, 1, None, :]).reshape(n, PEER_HEADS, PEER_TOPK * PEER_TOPK)
        top_s, pos = lax.top_k(cand, PEER_TOPK)
        eid = jnp.take_along_axis(cid, pos, axis=-1)
        gate = jax.nn.softmax(top_s, axis=-1)
        u = expert_u[eid]
        hid = jnp.einsum('nhkd,nd->nhk', u, xc).astype(jnp.float32)
        act = (jax.nn.gelu(hid, approximate=False) * gate).astype(xc.dtype)
        return jnp.einsum('nhk,nhkd->nd', act, expert_v[eid])

    return lax.map(block, xt).reshape(B, S, D)


def setup_inputs(seed: int = 0) -> dict:
    key = jax.random.key(seed)
    ks = jax.random.split(key, 24)
    f32 = jnp.float32
    D = D_MODEL
    nrm = lambda k, shape, s: jax.random.normal(k, shape, f32) * s
    return {
        "x": nrm(ks[0], (BATCH, SEQ, D), 1.0),
        "p": nrm(ks[1], (DEPTH, BATCH, SEQ, PLE_DIM), 1.0),
        "ln_emb_g": 1.0 + nrm(ks[2], (D,), 0.02),
        "ln_emb_b": nrm(ks[3], (D,), 0.02),
        "w_in": nrm(ks[4], (DEPTH, D, N_IN), D ** -0.5),
        "b_forget": 2.0 + nrm(ks[5], (DEPTH, FOX_HEADS), 0.5),
        "b_branch_gate": nrm(ks[6], (DEPTH, 2, D), 0.02),
        "w_ret_o": nrm(ks[7], (DEPTH, RET_V_W, D), RET_V_W ** -0.5),
        "w_fox_o": nrm(ks[8], (DEPTH, FOX_W, D), FOX_W ** -0.5),
        "w_out": nrm(ks[9], (DEPTH, D, D), BETA * D ** -0.5),
        "ln1_g": 1.0 + nrm(ks[10], (DEPTH, D), 0.02),
        "ln1_b": nrm(ks[11], (DEPTH, D), 0.02),
        "w_peer_q": nrm(ks[12], (DEPTH, D, PEER_HEADS * PEER_DQ), D ** -0.5),
        "peer_sub_keys": nrm(ks[13], (DEPTH, PEER_HEADS, 2, PEER_NKEYS, PEER_HALF), PEER_HALF ** -0.5),
        "peer_u": nrm(ks[14], (DEPTH, PEER_NEXP, D), D ** -0.5),
        "peer_v": nrm(ks[15], (DEPTH, PEER_NEXP, D), BETA * PEER_HEADS ** -0.5),
        "w_ple_gate": nrm(ks[16], (DEPTH, D, D), D ** -0.5),
        "b_ple_gate": nrm(ks[17], (DEPTH, D), 0.02),
        "w_ple": nrm(ks[18], (DEPTH, PLE_DIM, D), BETA * PLE_DIM ** -0.5),
        "ln2_g": 1.0 + nrm(ks[19], (DEPTH, D), 0.02),
        "ln2_b": nrm(ks[20], (DEPTH, D), 0.02),
    }


def reference(x, p, ln_emb_g, ln_emb_b, w_in, b_forget, b_branch_gate, w_ret_o, w_fox_o, w_out,
              ln1_g, ln1_b, w_peer_q, peer_sub_keys, peer_u, peer_v, w_ple_gate, b_ple_gate, w_ple,
              ln2_g, ln2_b):
    B, S, _ = x.shape
    pos = jnp.arange(S, dtype=jnp.int32)
    split_points = np.cumsum(np.array(IN_SPLITS))[:-1].tolist()
    h = layer_norm(x, ln_emb_g, ln_emb_b)
    for i in range(DEPTH):
        proj = h @ w_in[i]
        rq, rk, rv, rg, fq, fk, fv, ff, gr, gf = jnp.split(proj, split_points, axis=-1)
        q_r = rotary(rq.reshape(B, S, RET_HEADS, RET_DK), pos)
        k_r = rotary(rk.reshape(B, S, RET_HEADS, RET_DK), pos) * (RET_DK ** -0.5)
        v_r = rv.reshape(B, S, RET_HEADS, RET_DV).astype(jnp.float32)
        y_r = group_norm_heads(retention(q_r, k_r, v_r)).reshape(B, S, RET_V_W).astype(h.dtype)
        y_ret = (jax.nn.silu(rg) * y_r) @ w_ret_o[i]
        log_f = jax.nn.log_sigmoid(ff.astype(jnp.float32) + b_forget[i].astype(jnp.float32))
        y_f = forgetting_attention(fq.reshape(B, S, FOX_HEADS, FOX_DH),
                                   fk.reshape(B, S, FOX_HEADS, FOX_DH),
                                   fv.reshape(B, S, FOX_HEADS, FOX_DH), log_f)
        y_fox = y_f.astype(h.dtype) @ w_fox_o[i]
        merged = jax.nn.sigmoid(gr + b_branch_gate[i, 0]) * y_ret + jax.nn.sigmoid(gf + b_branch_gate[i, 1]) * y_fox
        h = layer_norm(ALPHA * h + merged @ w_out[i], ln1_g[i], ln1_b[i])
        ple = jax.nn.sigmoid(h @ w_ple_gate[i] + b_ple_gate[i]) * (p[i] @ w_ple[i])
        ch = peer(h, w_peer_q[i], peer_sub_keys[i], peer_u[i], peer_v[i]) + ple
        h = layer_norm(ALPHA * h + ch, ln2_g[i], ln2_b[i])
    return h
```

```python
import numpy as np
from contextlib import ExitStack
import concourse.bass as bass
import concourse.mybir as mybir
from concourse.bass_utils import run_bass_kernel_spmd

F32 = mybir.dt.float32
BF16 = mybir.dt.bfloat16
U32 = mybir.dt.uint32
I32 = mybir.dt.int32
AF = mybir.ActivationFunctionType
ALU = mybir.AluOpType
AX = mybir.AxisListType


class Buf:
    def __init__(self, t, name):
        self.t = t
        self.name = name
        self.writer = None
        self.readers = []
        self.dsem = None
        self.dcnt = 0
        self.ssem = None
        self.scnt = 0
        self.wtoks = {}


class Sched:
    ENG = ("pe", "act", "dve", "pool", "sp")

    def __init__(self, nc, es):
        self.nc = nc
        self.es = es
        self.es0 = es
        self.eng = {"pe": nc.tensor, "act": nc.scalar, "dve": nc.vector,
                    "pool": nc.gpsimd, "sp": nc.sync}
        self.sem = {e: es.enter_context(nc.semaphore("sem_" + e)) for e in self.ENG if e != "sp"}
        self.cnt = {e: 0 for e in self.ENG}
        self.seen = {e: {} for e in self.ENG}
        self.out_tokens = []
        self.nsem = 4
        self.all_dma = {}
        self.ninstr = 0

    def sb(self, name, shape, dtype):
        t = self.es.enter_context(self.nc.sbuf_tensor("s_" + name, list(shape), dtype))
        return Buf(t, name)

    def ps(self, name, shape, dtype=F32):
        t = self.es.enter_context(self.nc.psum_tensor(name, list(shape), dtype))
        return Buf(t, name)

    def newsem(self, name):
        self.nsem += 1
        return self.es0.enter_context(self.nc.semaphore(name))

    def _needs(self, reads, writes):
        need = []
        for b in reads:
            if b.writer is not None:
                need.append(b.writer)
        for b in writes:
            if b.writer is not None:
                need.append(b.writer)
            need.extend(b.readers)
        return need

    def _wait(self, E, need):
        best = {}
        for tok in need:
            kind, s, c = tok
            if kind == "e":
                if s == E and E == "pe":
                    continue
                key = ("e", s)
                semh = self.sem[s]
            else:
                key = ("d", id(s))
                semh = s
            if key not in best or best[key][1] < c:
                best[key] = (semh, c)
        for key, (semh, c) in best.items():
            if self.seen[E].get(key, 0) >= c:
                continue
            self.eng[E].wait_ge(semh, c)
            self.seen[E][key] = c

    def _update(self, tok, reads, writes):
        for b in writes:
            b.writer = tok
            b.readers = []
        for b in reads:
            if b in writes:
                continue
            b.readers = [r for r in b.readers
                         if not (r[0] == tok[0] and (r[1] == tok[1] if tok[0] == "e" else r[1] is tok[1]))]
            b.readers.append(tok)

    def op(self, E, fn, reads=(), writes=()):
        reads = list(reads)
        writes = list(writes)
        self._wait(E, self._needs(reads, writes))
        ins = fn(self.eng[E])
        self.cnt[E] += 1
        ins.then_inc(self.sem[E], 1)
        tok = ("e", E, self.cnt[E])
        self._update(tok, reads, writes)
        self.ninstr += 1
        return tok

    def pe(self, fn, reads=(), writes=()):
        return self.op("pe", fn, reads, writes)

    def a(self, fn, reads=(), writes=()):
        return self.op("act", fn, reads, writes)

    def v(self, fn, reads=(), writes=()):
        return self.op("dve", fn, reads, writes)

    def g(self, fn, reads=(), writes=()):
        return self.op("pool", fn, reads, writes)

    def pe_group(self, fns, reads=(), writes=()):
        reads = list(reads)
        writes = list(writes)
        self._wait("pe", self._needs(reads, writes))
        ins = None
        for fn in fns:
            ins = fn(self.eng["pe"])
            self.ninstr += 1
        self.cnt["pe"] += 1
        ins.then_inc(self.sem["pe"], 1)
        tok = ("e", "pe", self.cnt["pe"])
        self._update(tok, reads, writes)
        return tok

    def dma(self, out, in_, reads=(), writes=(), out_final=False, q="sp", parts=None):
        reads = list(reads)
        writes = list(writes)
        dram_w = [b for b in writes if b.t is None]
        dram_r = [b for b in reads if b.t is None]
        sreads = [b for b in reads if b.t is not None]
        swrites = [b for b in writes if b.t is not None]
        need = self._needs(sreads, swrites)
        for b in dram_r:
            need.extend(b.wtoks.values())
        self._wait(q, need)
        pairs = parts if parts is not None else [(out, in_)]
        if swrites:
            b = swrites[0]
            if b.dsem is None:
                b.dsem = self.newsem("ld_" + b.name)
            semh = b.dsem
            b.dcnt += 16 * len(pairs)
            c = b.dcnt
        else:
            b = sreads[0]
            if b.ssem is None:
                b.ssem = self.newsem("st_" + b.name)
            semh = b.ssem
            b.scnt += 16 * len(pairs)
            c = b.scnt
        for (o, i) in pairs:
            self.eng[q].dma_start(out=o, in_=i).then_inc(semh, 16)
            self.ninstr += 1
        tok = ("d", semh, c)
        self.all_dma[id(semh)] = tok
        self._update(tok, sreads, swrites)
        for b in dram_w:
            b.wtoks[id(semh)] = tok
        if out_final:
            self.out_tokens.append(tok)
        return tok

    def gather(self, out, table, idx, reads=(), writes=()):
        reads = list(reads)
        writes = list(writes)
        self._wait("pool", self._needs(reads, writes))
        b = writes[0]
        if b.dsem is None:
            b.dsem = self.newsem("ld_" + b.name)
        b.dcnt += 16
        self.nc.gpsimd.indirect_dma_start(
            out, None, table, bass.IndirectOffsetOnAxis(idx, 0)
        ).then_inc(b.dsem, 16)
        self.ninstr += 1
        tok = ("d", b.dsem, b.dcnt)
        self.all_dma[id(b.dsem)] = tok
        self._update(tok, reads, writes)
        return tok

    def finish(self):
        self._wait("sp", self.out_tokens)

    def barrier(self):
        toks = [("e", e, self.cnt[e]) for e in ("pe", "act", "dve", "pool") if self.cnt[e] > 0]
        toks += list(self.all_dma.values())
        for E in self.ENG:
            self._wait(E, toks)


D = 1024
N_IN = 6664
RET_H, RET_DK, RET_DV = 4, 128, 256
FOX_H, FOX_DH = 8, 64
PEER_H, PEER_K = 8, 16
LN_EPS = 1e-5
ALPHA = 2.0 ** 0.25
C_RQ, C_RK, C_RV, C_RG, C_FQ, C_FK, C_FV, C_FF, C_GR, C_GF = (
    0, 512, 1024, 2048, 3072, 3584, 4096, 4608, 4616, 5640)
NGATH = 6


def host_consts(S):
    c = {}
    c["ident_f"] = np.eye(128, dtype=np.float32)
    half = 64
    inv = 10000.0 ** (-np.arange(half, dtype=np.float32) / half)
    ang = np.arange(S, dtype=np.float32)[:, None] * inv[None, :]
    c["cos_t"] = np.cos(ang).astype(np.float32)
    c["sin_t"] = np.sin(ang).astype(np.float32)
    log_g = np.log(1.0 - 2.0 ** (-5.0 - np.arange(RET_H, dtype=np.float64)))
    idx = np.arange(128, dtype=np.float64)
    kscale = RET_DK ** -0.5
    mg = np.zeros((128, RET_H, 128), np.float64)
    for h in range(RET_H):
        m = (idx[None, :] >= idx[:, None]).astype(np.float64)
        mg[:, h, :] = m * np.exp(-log_g[h] * (idx[:, None] + 1.0)) * kscale
    c["maskg"] = mg.astype(np.float32)
    c["gpow"] = np.exp(log_g[None, :] * (idx[:, None] + 1.0)).astype(np.float32)
    c["kdec"] = (np.exp(log_g[None, :] * (127.0 - idx[:, None])) * kscale).astype(np.float32)
    c["causalT"] = (idx[None, :] >= idx[:, None]).astype(np.float32)
    sel = np.zeros((8, 8, 128), np.float32)
    for h in range(8):
        sel[h, h, :] = 1.0
    c["sel8"] = sel
    c["iota16"] = np.tile(np.arange(16, dtype=np.float32)[None, :], (128, 1))
    c["ones8"] = np.ones((8, 128), np.float32)
    c["iota128"] = np.tile(np.arange(128, dtype=np.float32)[None, :], (128, 1))
    cd = [float(np.exp(log_g[h] * 128.0)) for h in range(RET_H)]
    return c, cd


def build_program(NB, S, debug=False):
    NT = S // 128
    NBLK = NB * NT
    NTOK = NB * S
    nc = bass.Bass("TRN2", target_bir_lowering=False)
    _, CD = host_consts(128)

    def din(name, shape, dt=F32):
        return nc.dram_tensor(name, list(shape), dt, kind="ExternalInput").ap()

    x_d = din("x", [NTOK, D])
    p_d = din("p", [NTOK, 256])
    w_in_d = din("w_in", [D, N_IN])
    w_ret_o_d = din("w_ret_o", [1024, D])
    w_fox_o_d = din("w_fox_o", [512, D])
    w_out_d = din("w_out", [D, D])
    w_peer_q_d = din("w_peer_q", [D, 2048])
    keys_d = din("peer_sub_keys", [16, 128, 128])
    pu_d = din("peer_u", [16384, D])
    pv_d = din("peer_v", [16384, D])
    w_pg_d = din("w_ple_gate", [D, D])
    w_ple_d = din("w_ple", [256, D])
    rep = {n: din(n, [128, D]) for n in
           ("ln_emb_g", "ln_emb_b", "ln1_g", "ln1_b", "ln2_g", "ln2_b", "b_ple_gate")}
    bbr_d = din("bbr", [128, 2, 8])
    bfg_d = din("b_forget", [8, 1])
    ident_d = din("ident_f", [128, 128])
    cos_d = din("cos_t", [S, 64])
    sin_d = din("sin_t", [S, 64])
    maskg_d = din("maskg", [128, 4, 128])
    gpow_d = din("gpow", [128, 4])
    kdec_d = din("kdec", [128, 4])
    causal_d = din("causalT", [128, 128])
    sel8_d = din("sel8", [8, 8, 128])
    iota16_d = din("iota16", [128, 16])
    ones8_d = din("ones8", [8, 128])
    iota128_d = din("iota128", [128, 128])
    uts_d = nc.dram_tensor("uts", [64, 128, 2 * D], BF16, kind="Internal").ap()
    vs_d = nc.dram_tensor("vs", [64, 128, 2 * D], BF16, kind="Internal").ap()
    wins_d = nc.dram_tensor("wins", [13, 128, 8 * 512], BF16, kind="Internal").ap()
    h1ts_d = nc.dram_tensor("h1ts", [NBLK, 128, D], BF16, kind="Internal").ap()
    qts_d = nc.dram_tensor("qts", [NBLK, 128, 2048], BF16, kind="Internal").ap()
    zres_d = nc.dram_tensor("zres", [NTOK, D], F32, kind="Internal").ap()
    y_d = nc.dram_tensor("y", [NTOK, D], F32, kind="ExternalOutput").ap()
    if debug:
        h1s_d = nc.dram_tensor("h1s", [NTOK, D], F32, kind="ExternalOutput").ap()
    else:
        h1s_d = nc.dram_tensor("h1s", [NTOK, D], F32, kind="Internal").ap()

    def wview(w, c0, n):
        return w[:, c0:c0 + n].rearrange("(kc p) n -> p kc n", p=128)

    with ExitStack() as es0:
        S_ = Sched(nc, es0)
        h1s = Buf(None, "h1s")

        with ExitStack() as es1:
            S_.es = es1
            sb, ps = S_.sb, S_.ps
            wtiles = [("rq", C_RQ), ("rk", C_RK), ("rv0", C_RV), ("rv1", C_RV + 512), ("rg0", C_RG),
                      ("rg1", C_RG + 512), ("fq", C_FQ), ("fk", C_FK), ("fv", C_FV),
                      ("gr0", C_GR), ("gr1", C_GR + 512), ("gf0", C_GF), ("gf1", C_GF + 512)]
            WINs = Buf(None, "WINs")
            with ExitStack() as esw:
                S_.es = esw
                wst = [sb("wst%d" % i, [128, 8, 512], BF16) for i in range(3)]
                for k, (name, c0) in enumerate(wtiles):
                    slot = wst[k % 3]
                    S_.dma(slot.t[:], wview(w_in_d, c0, 512), writes=[slot], q="pool")
                    S_.dma(wins_d[k], slot.t[:].rearrange("p a b -> p (a b)"), reads=[slot], writes=[WINs])
                S_.barrier()
            S_.es = es1
            ident_f = sb("ident_f", [128, 128], F32)
            ident_b = sb("ident_b", [128, 128], BF16)
            cos_t = sb("cos_t", [128, NT, 64], F32)
            sin_t = sb("sin_t", [128, NT, 64], F32)
            maskg = sb("maskg", [128, 4, 128], F32)
            gpow = sb("gpow", [128, 4], F32)
            kdec = sb("kdec", [128, 4], F32)
            causal = sb("causal", [128, 128], BF16)
            sel8 = sb("sel8", [8, 8, 128], BF16)
            ones8 = sb("ones8", [8, 128], F32)
            g0 = sb("g0", [128, D], F32)
            b0 = sb("b0", [128, D], F32)
            g1 = sb("g1", [128, D], F32)
            b1 = sb("b1", [128, D], F32)
            bbr = sb("bbr", [128, 2, 8], F32)
            negb = sb("negb", [8, 1], F32)
            bfg = sb("bfg", [8, 1], F32)
            w_ret_o = sb("w_ret_o", [128, 8, D], BF16)
            w_fox_o = sb("w_fox_o", [128, 4, D], BF16)
            w_out = sb("w_out", [128, 8, D], BF16)
            wff = sb("wff", [128, 8, 8], BF16)

            S_.dma(ident_f.t[:], ident_d, writes=[ident_f])
            S_.dma(ident_b.t[:], ident_d, writes=[ident_b], q="pool")
            S_.dma(cos_t.t[:], cos_d.rearrange("(t p) c -> p t c", p=128), writes=[cos_t])
            S_.dma(sin_t.t[:], sin_d.rearrange("(t p) c -> p t c", p=128), writes=[sin_t])
            S_.dma(maskg.t[:], maskg_d, writes=[maskg])
            S_.dma(gpow.t[:], gpow_d, writes=[gpow])
            S_.dma(kdec.t[:], kdec_d, writes=[kdec])
            S_.dma(causal.t[:], causal_d, writes=[causal], q="pool")
            S_.dma(sel8.t[:], sel8_d, writes=[sel8], q="pool")
            S_.dma(ones8.t[:], ones8_d, writes=[ones8])
            S_.dma(g0.t[:], rep["ln_emb_g"], writes=[g0])
            S_.dma(b0.t[:], rep["ln_emb_b"], writes=[b0])
            S_.dma(g1.t[:], rep["ln1_g"], writes=[g1])
            S_.dma(b1.t[:], rep["ln1_b"], writes=[b1])
            S_.dma(bbr.t[:], bbr_d, writes=[bbr])
            S_.dma(bfg.t[:], bfg_d, writes=[bfg])
            S_.v(lambda e: e.tensor_scalar_mul(negb.t[:], bfg.t[:], -1.0), reads=[bfg], writes=[negb])
            S_.dma(None, None, writes=[w_ret_o], q="pool",
                   parts=[(w_ret_o.t[:, :, h * 512:(h + 1) * 512], wview(w_ret_o_d, h * 512, 512)) for h in range(2)])
            S_.dma(None, None, writes=[w_fox_o], q="pool",
                   parts=[(w_fox_o.t[:, :, h * 512:(h + 1) * 512], wview(w_fox_o_d, h * 512, 512)) for h in range(2)])
            S_.dma(None, None, writes=[w_out], q="pool",
                   parts=[(w_out.t[:, :, h * 512:(h + 1) * 512], wview(w_out_d, h * 512, 512)) for h in range(2)])
            S_.dma(wff.t[:], wview(w_in_d, C_FF, 8), writes=[wff], q="pool")

            KTc = sb("KTc", [128, 4, S], BF16)
            Vaug = sb("Vaug", [128, NT, 8, 65], BF16)
            negcTM = sb("negcTM", [128, NT, 8], F32)
            Rst = sb("Rst", [128, 4, 256], F32)
            Rb = sb("Rb", [128, 4, 256], BF16)
            carry = sb("carry", [8, 1], F32)
            S_.v(lambda e: e.memset(Vaug.t[:], 1.0), writes=[Vaug])
            epsb = sb("epsb", [128, 1], F32)
            S_.v(lambda e: e.memset(epsb.t[:], LN_EPS), writes=[epsb])

            xt = [sb("xt%d" % i, [128, D], F32) for i in range(2)]
            NWR = 4
            wr = [sb("wr%d" % i, [128, 8, 512], BF16) for i in range(NWR)]
            stats = sb("stats", [128, 4, 6], F32)
            mv = sb("mv", [128, 4, 2], F32)
            rstd = sb("rstd", [128, 4], F32)
            h0r = [sb("h0_%d" % i, [128, D], F32) for i in range(2)]
            h0b = sb("h0b", [128, D], BF16)
            hT = sb("hT", [128, 8, 128], BF16)
            qk_rot = sb("qk_rot", [128, 2, 512], BF16)
            rt = [sb("rt%d" % i, [128, 4, 64], F32) for i in range(4)]
            qkT = sb("qkT", [128, 8, 128], BF16)
            kdd = sb("kdd", [128, 4, 128], BF16)
            Vr = sb("Vr", [128, D], BF16)
            srg = sb("srg", [128, D], BF16)
            fqT = sb("fqT", [128, 4, 128], BF16)
            grT = sb("grT", [128, 8, 128], BF16)
            gfT = sb("gfT", [128, 8, 128], BF16)
            ee = sb("ee", [8, 128], F32)
            nlf = sb("nlf", [8, 128], F32)
            negc = sb("negc", [8, 128], F32)
            cqhi = sb("cqhi", [8, 128], BF16)
            sTr = [sb("sT%d" % i, [128, 128], BF16) for i in range(4)]
            ret = sb("ret", [128, D], F32)
            yg = sb("yg", [128, D], BF16)
            ygT = sb("ygT", [128, 8, 128], BF16)
            PTr = [sb("PTr%d" % i, [128, 128], BF16) for i in range(4)]
            rinv = sb("rinv", [128, 8, 1], F32)
            yf = sb("yf", [128, 8, 64], BF16)
            yfT = sb("yfT", [128, 4, 128], BF16)
            t1 = sb("t1", [128, 8, 128], F32)
            t2 = sb("t2", [128, 8, 128], F32)
            mgd = sb("mgd", [128, 8, 128], BF16)
            h1 = sb("h1", [128, D], F32)

            PA_t = es1.enter_context(nc.psum_tensor("PA", [128, 1024], F32))
            PB_t = es1.enter_context(nc.psum_tensor("PB", [128, 1024], F32))
            PC_t = es1.enter_context(nc.psum_tensor("PC", [128, 1024], F32))
            PD_t = es1.enter_context(nc.psum_tensor("PD", [128, 512], F32))
            PT_t = es1.enter_context(nc.psum_tensor("PT", [128, 1024], BF16))
            PA = [Buf(PA_t, "PA0"), Buf(PA_t, "PA1")]
            PB = [Buf(PB_t, "PB0"), Buf(PB_t, "PB1")]
            PC = [Buf(PC_t, "PC0"), Buf(PC_t, "PC1")]
            PD = [Buf(PD_t, "PD%d" % i) for i in range(4)]
            PT = Buf(PT_t, "PT")

            def PAv(i, n=512):
                return PA_t[:, i * 512:i * 512 + n]

            def layer_norm(src, dst, g, b):
                for c in range(2):
                    S_.v(lambda e, c=c: e.bn_stats(stats.t[:, c, :], src.t[:, c * 512:(c + 1) * 512]),
                         reads=[src], writes=[stats])
                S_.v(lambda e: e.bn_aggr(mv.t[:, 0, :], stats.t[:, 0:2, :].rearrange("p a b -> p (a b)")), reads=[stats], writes=[mv])
                S_.a(lambda e: e.activation(rstd.t[:, 0:1], mv.t[:, 0, 1:2], AF.Ln, bias=epsb.t[:, 0:1]),
                     reads=[mv, epsb], writes=[rstd])
                S_.a(lambda e: e.activation(rstd.t[:, 0:1], rstd.t[:, 0:1], AF.Exp, scale=-0.5),
                     reads=[rstd], writes=[rstd])
                S_.v(lambda e: e.tensor_scalar(dst.t[:], src.t[:], mv.t[:, 0, 0:1], rstd.t[:, 0:1],
                                               ALU.subtract, ALU.mult), reads=[src, mv, rstd], writes=[dst])
                S_.v(lambda e: e.tensor_mul(dst.t[:], dst.t[:], g.t[:]), reads=[dst, g], writes=[dst])
                S_.v(lambda e: e.tensor_add(dst.t[:], dst.t[:], b.t[:]), reads=[dst, b], writes=[dst])

            def transpose_to(src, nchunk, dst, ident, extra_reads=()):
                src2 = src.t[:] if len(src.t.shape) == 2 else src.t[:].rearrange("p a b -> p (a b)")
                S_.pe_group([lambda e, c=c: e.transpose(PT_t[:, c * 128:(c + 1) * 128],
                                                        src2[:, c * 128:(c + 1) * 128], ident.t[:])
                             for c in range(nchunk)], reads=[src, ident], writes=[PT])
                S_.a(lambda e: e.copy(dst.t[:].rearrange("p c t -> p (c t)"), PT_t[:, 0:nchunk * 128]),
                     reads=[PT], writes=[dst])

            wcount = [0]


            def load_wtile(c0):
                k = [kk for kk, (nm, cc) in enumerate(wtiles) if cc == c0][0]
                slot = wr[wcount[0] % NWR]
                wcount[0] += 1
                S_.dma(slot.t[:].rearrange("p a b -> p (a b)"), wins_d[k], reads=[WINs], writes=[slot])
                return slot

            pa_i = [0]

            def mm_tm(w):
                i = pa_i[0] % 2
                pa_i[0] += 1
                S_.pe_group([lambda e, kc=kc, i=i: e.matmul(PAv(i), hT.t[:, kc, :], w.t[:, kc, :],
                                                            start=(kc == 0), stop=(kc == 7))
                             for kc in range(8)], reads=[hT, w], writes=[PA[i]])
                return PA[i], PAv(i)

            def mm_fm(w):
                i = pa_i[0] % 2
                pa_i[0] += 1
                fns = []
                for m in range(4):
                    for kc in range(8):
                        fns.append(lambda e, kc=kc, m=m, i=i: e.matmul(
                            PA_t[:, i * 512 + m * 128:i * 512 + (m + 1) * 128],
                            w.t[:, kc, m * 128:(m + 1) * 128], hT.t[:, kc, :],
                            start=(kc == 0), stop=(kc == 7)))
                S_.pe_group(fns, reads=[hT, w], writes=[PA[i]])
                return PA[i], PAv(i)

            print("pass1 sbuf remaining", nc.sbuf_bytes_remaining)
            def front(blk):
                xb = xt[blk % 2]
                h0_ = h0r[blk % 2]
                S_.dma(xb.t[:], x_d[blk * 128:(blk + 1) * 128, :], writes=[xb])
                layer_norm(xb, h0_, g0, b0)
                S_.a(lambda e: e.copy(h0b.t[:], h0_.t[:]), reads=[h0_], writes=[h0b])
                transpose_to(h0b, 8, hT, ident_b)

            front(0)
            for blk in range(NBLK):
                bi, t = divmod(blk, NT)
                h0 = h0r[blk % 2]
                if t == 0:
                    S_.v(lambda e: e.memset(Rst.t[:], 0.0), writes=[Rst])
                    S_.v(lambda e: e.memset(Rb.t[:], 0.0), writes=[Rb])

                cosb = cos_t.t[:, t, :].unsqueeze(1).to_broadcast([128, 4, 64])
                sinb = sin_t.t[:, t, :].unsqueeze(1).to_broadcast([128, 4, 64])
                for name, c0 in wtiles:
                    w = load_wtile(c0)
                    if name in ("rq", "rk"):
                        pb, pv = mm_tm(w)
                        qi = 0 if name == "rq" else 1
                        pv4 = pv.rearrange("p (h c d) -> p h c d", h=4, c=2)
                        x1, x2 = pv4[:, :, 0, :], pv4[:, :, 1, :]
                        o4 = qk_rot.t[:, qi, :].rearrange("p (h c d) -> p h c d", h=4, c=2)
                        S_.v(lambda e: e.tensor_mul(rt[0].t[:], x1, cosb), reads=[pb, cos_t], writes=[rt[0]])
                        S_.v(lambda e: e.tensor_mul(rt[1].t[:], x2, sinb), reads=[pb, sin_t], writes=[rt[1]])
                        S_.v(lambda e: e.tensor_mul(rt[2].t[:], x2, cosb), reads=[pb, cos_t], writes=[rt[2]])
                        S_.v(lambda e: e.tensor_mul(rt[3].t[:], x1, sinb), reads=[pb, sin_t], writes=[rt[3]])
                        S_.v(lambda e: e.tensor_sub(o4[:, :, 0, :], rt[0].t[:], rt[1].t[:]),
                             reads=[rt[0], rt[1]], writes=[qk_rot])
                        S_.v(lambda e: e.tensor_add(o4[:, :, 1, :], rt[2].t[:], rt[3].t[:]),
                             reads=[rt[2], rt[3]], writes=[qk_rot])
                    elif name in ("rv0", "rv1"):
                        pb, pv = mm_tm(w)
                        o = 0 if name == "rv0" else 512
                        S_.a(lambda e: e.copy(Vr.t[:, o:o + 512], pv), reads=[pb], writes=[Vr])
                    elif name in ("rg0", "rg1"):
                        pb, pv = mm_tm(w)
                        o = 0 if name == "rg0" else 512
                        S_.a(lambda e: e.activation(srg.t[:, o:o + 512], pv, AF.Silu), reads=[pb], writes=[srg])
                    elif name == "fv":
                        pb, pv = mm_tm(w)
                        S_.a(lambda e: e.copy(Vaug.t[:, t, :, 0:64], pv.rearrange("p (h d) -> p h d", h=8)),
                             reads=[pb], writes=[Vaug])
                    elif name == "fq":
                        pb, pv = mm_fm(w)
                        S_.a(lambda e: e.mul(fqT.t[:].rearrange("p c t -> p (c t)"), pv, FOX_DH ** -0.5),
                             reads=[pb], writes=[fqT])
                    elif name == "fk":
                        pb, pv = mm_fm(w)
                        S_.a(lambda e: e.copy(KTc.t[:, :, t * 128:(t + 1) * 128],
                                              pv.rearrange("p (c t) -> p c t", c=4)), reads=[pb], writes=[KTc])
                    else:
                        pb, pv = mm_fm(w)
                        gi = 0 if name[1] == "r" else 1
                        half = int(name[2])
                        dst = grT if gi == 0 else gfT
                        for m in range(4):
                            S_.a(lambda e, m=m: e.activation(dst.t[:, half * 4 + m, :], pv[:, m * 128:(m + 1) * 128],
                                                            AF.Sigmoid, bias=bbr.t[:, gi, half * 4 + m:half * 4 + m + 1]),
                                 reads=[pb, bbr], writes=[dst])
                i = pa_i[0] % 2
                pa_i[0] += 1
                S_.pe_group([lambda e, kc=kc, i=i: e.matmul(PA_t[0:8, i * 512:i * 512 + 128], wff.t[:, kc, :],
                                                            hT.t[:, kc, :], start=(kc == 0), stop=(kc == 7))
                             for kc in range(8)], reads=[hT, wff], writes=[PA[i]])
                S_.a(lambda e, i=i: e.activation(ee.t[:], PA_t[0:8, i * 512:i * 512 + 128], AF.Exp,
                                                 bias=negb.t[:, 0:1], scale=-1.0), reads=[PA[i], negb], writes=[ee])
                S_.a(lambda e: e.activation(nlf.t[:], ee.t[:], AF.Ln, bias=1.0), reads=[ee], writes=[nlf])
                if t == 0:
                    S_.v(lambda e: e.tensor_tensor_scan(negc.t[:], ones8.t[:], nlf.t[:], 0.0, ALU.mult, ALU.add),
                         reads=[ones8, nlf], writes=[negc])
                else:
                    S_.v(lambda e: e.tensor_tensor_scan(negc.t[:], ones8.t[:], nlf.t[:], carry.t[:, 0:1],
                                                        ALU.mult, ALU.add), reads=[ones8, nlf, carry], writes=[negc])
                S_.v(lambda e: e.tensor_copy(carry.t[:], negc.t[:, 127:128]), reads=[negc], writes=[carry])
                S_.v(lambda e: e.tensor_scalar_mul(cqhi.t[:], negc.t[:], -1.0), reads=[negc], writes=[cqhi])
                S_.pe(lambda e: e.transpose(PD_t[:, 0:8], negc.t[:], ident_f.t[0:8, 0:8]),
                      reads=[negc, ident_f], writes=[PD[0]])
                S_.v(lambda e: e.tensor_copy(negcTM.t[:, t, :], PD_t[:, 0:8]), reads=[PD[0]], writes=[negcTM])

                transpose_to(qk_rot, 8, qkT, ident_b)
                S_.v(lambda e: e.tensor_mul(kdd.t[:], qk_rot.t[:, 1, :].rearrange("p (h d) -> p h d", h=4),
                                            kdec.t[:].unsqueeze(2).to_broadcast([128, 4, 128])),
                     reads=[qk_rot, kdec], writes=[kdd])
                PTf = PT_t.bitcast(F32)
                st_b = [(PB[0], PB_t[:, 0:128]), (PB[1], PB_t[:, 512:640]), (PD[0], PD_t[:, 0:128]), (PA[0], PA_t[:, 0:128])]
                ou_b = [(PA[1], PA_t[:, 512:768]), (PC[0], PC_t[:, 0:256]), (PC[1], PC_t[:, 512:768]), (PT, PTf[:, 0:256])]
                dr_b = [(PB[0], PB_t[:, 0:256]), (PB[1], PB_t[:, 512:768]), (PD[0], PD_t[:, 0:256]), (PA[0], PA_t[:, 0:256])]
                for h in range(4):
                    S_.pe(lambda e, h=h: e.matmul(st_b[h][1], qkT.t[:, 4 + h, :], qkT.t[:, h, :],
                                                  start=True, stop=True), reads=[qkT], writes=[st_b[h][0]])
                for h in range(4):
                    S_.v(lambda e, h=h: e.tensor_mul(sTr[h].t[:], st_b[h][1], maskg.t[:, h, :]),
                         reads=[st_b[h][0], maskg], writes=[sTr[h]])
                for h in range(4):
                    S_.pe_group([
                        lambda e, h=h: e.matmul(ou_b[h][1], sTr[h].t[:], Vr.t[:, h * 256:(h + 1) * 256],
                                                start=True, stop=False),
                        lambda e, h=h: e.matmul(ou_b[h][1], qkT.t[:, h, :], Rb.t[:, h, :],
                                                start=False, stop=True)],
                        reads=[sTr[h], Vr, qkT, Rb], writes=[ou_b[h][0]])
                for h in range(4):
                    S_.v(lambda e, h=h: e.tensor_scalar_mul(ret.t[:, h * 256:(h + 1) * 256], ou_b[h][1], gpow.t[:, h:h + 1]),
                         reads=[ou_b[h][0], gpow], writes=[ret])
                for h in range(4):
                    S_.pe(lambda e, h=h: e.matmul(dr_b[h][1], kdd.t[:, h, :], Vr.t[:, h * 256:(h + 1) * 256],
                                                  start=True, stop=True), reads=[kdd, Vr], writes=[dr_b[h][0]])
                for h in range(4):
                    S_.v(lambda e, h=h: e.scalar_tensor_tensor(Rst.t[:, h, :], Rst.t[:, h, :], CD[h],
                                                               dr_b[h][1], ALU.mult, ALU.add),
                         reads=[Rst, dr_b[h][0]], writes=[Rst])
                S_.a(lambda e: e.copy(Rb.t[:], Rst.t[:]), reads=[Rst], writes=[Rb])
                for h in range(4):
                    S_.v(lambda e, h=h: e.bn_stats(stats.t[:, h, :], ret.t[:, h * 256:(h + 1) * 256]), reads=[ret], writes=[stats])
                for h in range(4):
                    S_.v(lambda e, h=h: e.bn_aggr(mv.t[:, h, :], stats.t[:, h, :]), reads=[stats], writes=[mv])
                S_.a(lambda e: e.activation(rstd.t[:], mv.t[:, :, 1], AF.Ln, bias=epsb.t[:, 0:1]),
                     reads=[mv, epsb], writes=[rstd])
                S_.a(lambda e: e.activation(rstd.t[:], rstd.t[:], AF.Exp, scale=-0.5), reads=[rstd], writes=[rstd])
                for h in range(4):
                    S_.v(lambda e, h=h: e.tensor_scalar(ret.t[:, h * 256:(h + 1) * 256], ret.t[:, h * 256:(h + 1) * 256],
                                                        mv.t[:, h, 0:1], rstd.t[:, h:h + 1], ALU.subtract, ALU.mult),
                         reads=[ret, mv, rstd], writes=[ret])
                S_.v(lambda e: e.tensor_mul(yg.t[:], ret.t[:], srg.t[:]),
                     reads=[ret, srg], writes=[yg])

                if blk + 1 < NBLK:
                    front(blk + 1)
                pairs = [(h, j) for h in range(8) for j in range(t + 1)]
                st_ring = [(PD[0], PD_t[:, 0:128]), (PA[0], PA_t[:, 0:128]),
                           (PA[1], PA_t[:, 512:640]), (PB[0], PB_t[:, 0:128])]
                LAF = 2
                for idx in range(len(pairs) + LAF):
                    if idx < len(pairs):
                        h, j = pairs[idx]
                        ch, hp = h // 2, (h % 2) * 64
                        pd, pd_ap = st_ring[idx % 4]
                        ptb = PTr[idx % 4]
                        S_.pe_group([
                            lambda e, pd_ap=pd_ap, ch=ch, hp=hp, j=j: e.matmul(
                                pd_ap, KTc.t[hp:hp + 64, ch, j * 128:(j + 1) * 128], fqT.t[hp:hp + 64, ch, :],
                                start=True, stop=False),
                            lambda e, pd_ap=pd_ap, h=h: e.matmul(pd_ap, sel8.t[:, h, :], cqhi.t[:],
                                                                 start=False, stop=True)],
                            reads=[KTc, fqT, sel8, cqhi], writes=[pd])
                        S_.a(lambda e, pd_ap=pd_ap, ptb=ptb, j=j, h=h: e.activation(
                            ptb.t[:], pd_ap, AF.Exp, bias=negcTM.t[:, j, h:h + 1]),
                            reads=[pd, negcTM], writes=[ptb])
                        if j == t:
                            S_.v(lambda e, ptb=ptb: e.tensor_mul(ptb.t[:], ptb.t[:], causal.t[:]),
                                 reads=[ptb, causal], writes=[ptb])
                    if idx >= LAF:
                        h, j = pairs[idx - LAF]
                        ptb = PTr[(idx - LAF) % 4]
                        acc = PC[h // 4]
                        acc_ap = PC_t[:, (h // 4) * 512 + (h % 4) * 65:(h // 4) * 512 + (h % 4) * 65 + 65]
                        S_.pe(lambda e, acc_ap=acc_ap, ptb=ptb, j=j, h=h: e.matmul(
                            acc_ap, ptb.t[:], Vaug.t[:, j, h, :], start=(j == 0), stop=(j == t)),
                            reads=[ptb, Vaug] + ([acc] if j > 0 else []), writes=[acc])
                transpose_to(yg, 8, ygT, ident_b)
                for half in range(2):
                    accv = PC_t[:, half * 512:half * 512 + 260].rearrange("p (h d) -> p h d", h=4)
                    S_.v(lambda e, accv=accv, half=half: e.reciprocal(rinv.t[:, half * 4:(half + 1) * 4, :],
                                                                      accv[:, :, 64:65]),
                         reads=[PC[half]], writes=[rinv])
                    S_.v(lambda e, accv=accv, half=half: e.tensor_mul(
                        yf.t[:, half * 4:(half + 1) * 4, :], accv[:, :, 0:64],
                        rinv.t[:, half * 4:(half + 1) * 4, :].to_broadcast([128, 4, 64])),
                        reads=[PC[half], rinv], writes=[yf])
                S_.pe_group([lambda e, c=c: e.transpose(PT_t[:, c * 128:(c + 1) * 128],
                                                        yf.t[:].rearrange("p h d -> p (h d)")[:, c * 128:(c + 1) * 128],
                                                        ident_b.t[:]) for c in range(4)],
                            reads=[yf, ident_b], writes=[PT])
                S_.a(lambda e: e.copy(yfT.t[:].rearrange("p c t -> p (c t)"), PT_t[:, 0:512]),
                     reads=[PT], writes=[yfT])

                fns = []
                for m in range(8):
                    for kc in range(8):
                        fns.append(lambda e, m=m, kc=kc: e.matmul(PA_t[:, m * 128:(m + 1) * 128],
                                                                  w_ret_o.t[:, kc, m * 128:(m + 1) * 128],
                                                                  ygT.t[:, kc, :], start=(kc == 0), stop=(kc == 7)))
                S_.pe_group(fns, reads=[w_ret_o, ygT], writes=[PA[0], PA[1]])
                fns = []
                for m in range(8):
                    for kc in range(4):
                        fns.append(lambda e, m=m, kc=kc: e.matmul(PB_t[:, m * 128:(m + 1) * 128],
                                                                  w_fox_o.t[:, kc, m * 128:(m + 1) * 128],
                                                                  yfT.t[:, kc, :], start=(kc == 0), stop=(kc == 3)))
                S_.pe_group(fns, reads=[w_fox_o, yfT], writes=[PB[0], PB[1]])
                S_.v(lambda e: e.tensor_mul(t1.t[:].rearrange("p c t -> p (c t)"), PA_t[:, :],
                                            grT.t[:].rearrange("p c t -> p (c t)")),
                     reads=[PA[0], PA[1], grT], writes=[t1])
                S_.v(lambda e: e.tensor_mul(t2.t[:].rearrange("p c t -> p (c t)"), PB_t[:, :],
                                            gfT.t[:].rearrange("p c t -> p (c t)")),
                     reads=[PB[0], PB[1], gfT], writes=[t2])
                S_.v(lambda e: e.tensor_add(mgd.t[:], t1.t[:], t2.t[:]), reads=[t1, t2], writes=[mgd])
                for n in range(2):
                    S_.pe_group([lambda e, kc=kc, n=n: e.matmul(PAv(n), mgd.t[:, kc, :],
                                                                w_out.t[:, kc, n * 512:(n + 1) * 512],
                                                                start=(kc == 0), stop=(kc == 7)) for kc in range(8)],
                                reads=[mgd, w_out], writes=[PA[n]])
                zf = ret.t[:]
                S_.v(lambda e: e.scalar_tensor_tensor(zf, h0.t[:], ALPHA, PA_t[:, :], ALU.mult, ALU.add),
                     reads=[h0, PA[0], PA[1]], writes=[ret])
                layer_norm(ret, h1, g1, b1)
                S_.dma(h1s_d[blk * 128:(blk + 1) * 128, :], h1.t[:], reads=[h1], writes=[h1s])
            S_.barrier()

        UTs = Buf(None, "UTs")
        Vs = Buf(None, "Vs")
        H1T_s = Buf(None, "H1T_s")
        QT_s = Buf(None, "QT_s")
        ZR_s = Buf(None, "ZR_s")
        with ExitStack() as es15:
            S_.es = es15
            sb = S_.sb
            ident_b = sb("ident_b15", [128, 128], BF16)
            bpg = sb("bpg", [128, D], F32)
            w_pq = sb("w_pq", [128, 8, 2048], BF16)
            w_pg = sb("w_pg", [128, 8, D], BF16)
            w_pl = sb("w_pl", [128, 2, D], BF16)
            S_.dma(ident_b.t[:], ident_d, writes=[ident_b], q="pool")
            S_.dma(bpg.t[:], rep["b_ple_gate"], writes=[bpg])
            S_.dma(None, None, writes=[w_pq], q="pool",
                   parts=[(w_pq.t[:, :, h * 512:(h + 1) * 512], wview(w_peer_q_d, h * 512, 512)) for h in range(4)])
            S_.dma(None, None, writes=[w_pg], q="pool",
                   parts=[(w_pg.t[:, :, h * 512:(h + 1) * 512], wview(w_pg_d, h * 512, 512)) for h in range(2)])
            S_.dma(w_pl.t[:], w_ple_d.rearrange("(kc p) n -> p kc n", p=128), writes=[w_pl], q="pool")
            PTa_t = es15.enter_context(nc.psum_tensor("PTa", [128, 1024], BF16))
            PTb_t = es15.enter_context(nc.psum_tensor("PTb", [128, 1024], BF16))
            PQa_t = es15.enter_context(nc.psum_tensor("PQa", [128, 2048], F32))
            PGa_t = es15.enter_context(nc.psum_tensor("PGa", [128, 1024], F32))
            PTs = [(Buf(PTa_t, "PTa"), PTa_t), (Buf(PTb_t, "PTb"), PTb_t)]
            PQa = [Buf(PQa_t, "PQa%d" % i) for i in range(4)]
            PGa = [Buf(PGa_t, "PGa%d" % i) for i in range(2)]
            h1r = [sb("a_h1r%d" % i, [128, D], F32) for i in range(2)]
            pr = [sb("a_pr%d" % i, [128, 256], F32) for i in range(2)]
            h1b = [sb("a_h1b%d" % i, [128, D], BF16) for i in range(2)]
            pbb = [sb("a_pbb%d" % i, [128, 256], BF16) for i in range(2)]
            hTa = [sb("a_hT%d" % i, [128, 8, 128], BF16) for i in range(2)]
            pTa = [sb("a_pT%d" % i, [128, 2, 128], BF16) for i in range(2)]
            qTa = [sb("a_qT%d" % i, [128, 16, 128], BF16) for i in range(2)]
            gta = [sb("a_gt%d" % i, [128, D], F32) for i in range(2)]
            zra = [sb("a_zr%d" % i, [128, D], F32) for i in range(2)]
            ub = [sb("ub%d" % i, [128, D], BF16) for i in range(3)]
            utb = [sb("utb%d" % i, [128, D], BF16) for i in range(2)]
            vb = [sb("vb%d" % i, [128, D], BF16) for i in range(3)]

            def prep_chunk(i):
                u_ = ub[i % 3]
                v_ = vb[i % 3]
                ut_ = utb[i % 2]
                ptb_, pt_t = PTs[i % 2]
                S_.dma(u_.t[:], pu_d[i * 128:(i + 1) * 128, :], writes=[u_], q="pool")
                S_.dma(v_.t[:], pv_d[i * 128:(i + 1) * 128, :], writes=[v_], q="pool")
                S_.pe_group([lambda e, c=c, u_=u_: e.transpose(pt_t[:, c * 128:(c + 1) * 128],
                                                                u_.t[:, c * 128:(c + 1) * 128], ident_b.t[:])
                             for c in range(8)], reads=[u_, ident_b], writes=[ptb_])
                S_.a(lambda e, ut_=ut_: e.copy(ut_.t[:], pt_t[:, :]), reads=[ptb_], writes=[ut_])
                S_.dma(uts_d[i // 2, :, (i % 2) * D:(i % 2 + 1) * D], ut_.t[:], reads=[ut_], writes=[UTs])
                S_.dma(vs_d[i // 2, :, (i % 2) * D:(i % 2 + 1) * D], v_.t[:], reads=[v_], writes=[Vs])

            prep_todo = list(range(128))
            prep_per_blk = -(-128 // NBLK)
            for blk in range(NBLK):
                par = blk % 2
                hb, pbk, hb16, pb16, hT_, pT_, qT_, gt_, zr_ = (h1r[par], pr[par], h1b[par], pbb[par], hTa[par],
                                                                pTa[par], qTa[par], gta[par], zra[par])
                ptb_, pt_t = PTs[par]
                S_.dma(hb.t[:], h1s_d[blk * 128:(blk + 1) * 128, :], reads=[h1s], writes=[hb])
                S_.dma(pbk.t[:], p_d[blk * 128:(blk + 1) * 128, :], writes=[pbk])
                S_.a(lambda e: e.copy(hb16.t[:], hb.t[:]), reads=[hb], writes=[hb16])
                S_.a(lambda e: e.copy(pb16.t[:], pbk.t[:]), reads=[pbk], writes=[pb16])
                S_.pe_group([lambda e, c=c: e.transpose(pt_t[:, c * 128:(c + 1) * 128],
                                                        hb16.t[:, c * 128:(c + 1) * 128], ident_b.t[:])
                             for c in range(8)], reads=[hb16, ident_b], writes=[ptb_])
                S_.v(lambda e: e.tensor_copy(hT_.t[:].rearrange("p c t -> p (c t)"), pt_t[:, :]),
                     reads=[ptb_], writes=[hT_])
                S_.pe_group([lambda e, c=c: e.transpose(pt_t[:, c * 128:(c + 1) * 128],
                                                        pb16.t[:, c * 128:(c + 1) * 128], ident_b.t[:])
                             for c in range(2)], reads=[pb16, ident_b], writes=[ptb_])
                S_.v(lambda e: e.tensor_copy(pT_.t[:].rearrange("p c t -> p (c t)"), pt_t[:, 0:256]),
                     reads=[ptb_], writes=[pT_])
                S_.dma(h1ts_d[blk], hT_.t[:].rearrange("p c t -> p (c t)"), reads=[hT_], writes=[H1T_s])
                for qc in range(4):
                    fns = []
                    for g4 in range(4):
                        ch = qc * 4 + g4
                        for kc in range(8):
                            fns.append(lambda e, ch=ch, kc=kc: e.matmul(
                                PQa_t[:, ch * 128:(ch + 1) * 128], w_pq.t[:, kc, ch * 128:(ch + 1) * 128],
                                hT_.t[:, kc, :], start=(kc == 0), stop=(kc == 7)))
                    S_.pe_group(fns, reads=[w_pq, hT_], writes=[PQa[qc]])
                    if qc % 2 == 0:
                        S_.a(lambda e, qc=qc: e.copy(qT_.t[:, qc * 4:(qc + 1) * 4, :].rearrange("p c t -> p (c t)"),
                                                     PQa_t[:, qc * 512:(qc + 1) * 512]), reads=[PQa[qc]], writes=[qT_])
                    else:
                        S_.v(lambda e, qc=qc: e.tensor_copy(qT_.t[:, qc * 4:(qc + 1) * 4, :].rearrange("p c t -> p (c t)"),
                                                            PQa_t[:, qc * 512:(qc + 1) * 512]), reads=[PQa[qc]], writes=[qT_])
                S_.dma(qts_d[blk], qT_.t[:].rearrange("p c t -> p (c t)"), reads=[qT_], writes=[QT_s])
                for _ in range(prep_per_blk):
                    if prep_todo:
                        prep_chunk(prep_todo.pop(0))
                for n in range(2):
                    S_.pe_group([lambda e, kc=kc, n=n: e.matmul(PGa_t[:, 0:512], hT_.t[:, kc, :],
                                                                w_pg.t[:, kc, n * 512:(n + 1) * 512],
                                                                start=(kc == 0), stop=(kc == 7)) for kc in range(8)],
                                reads=[hT_, w_pg], writes=[PGa[0]])
                    S_.pe_group([lambda e, kc=kc, n=n: e.matmul(PGa_t[:, 512:1024], pT_.t[:, kc, :],
                                                                w_pl.t[:, kc, n * 512:(n + 1) * 512],
                                                                start=(kc == 0), stop=(kc == 1)) for kc in range(2)],
                                reads=[pT_, w_pl], writes=[PGa[1]])
                    S_.v(lambda e, n=n: e.tensor_add(gt_.t[:, n * 512:(n + 1) * 512], PGa_t[:, 0:512],
                                                     bpg.t[:, n * 512:(n + 1) * 512]), reads=[PGa[0], bpg], writes=[gt_])
                    S_.a(lambda e, n=n: e.activation(gt_.t[:, n * 512:(n + 1) * 512], gt_.t[:, n * 512:(n + 1) * 512],
                                                     AF.Sigmoid), reads=[gt_], writes=[gt_])
                    S_.v(lambda e, n=n: e.tensor_mul(gt_.t[:, n * 512:(n + 1) * 512], gt_.t[:, n * 512:(n + 1) * 512],
                                                     PGa_t[:, 512:1024]), reads=[gt_, PGa[1]], writes=[gt_])
                S_.v(lambda e: e.scalar_tensor_tensor(zr_.t[:], hb.t[:], ALPHA, gt_.t[:], ALU.mult, ALU.add),
                     reads=[hb, gt_], writes=[zr_])
                S_.dma(zres_d[blk * 128:(blk + 1) * 128, :], zr_.t[:], reads=[zr_], writes=[ZR_s])
            while prep_todo:
                prep_chunk(prep_todo.pop(0))
            S_.barrier()

        with ExitStack() as es2:
            S_.es = es2
            sb = S_.sb
            NTILE = NBLK // 2
            ident_b = sb("ident_b2", [128, 128], BF16)
            ident_f = sb("ident_f2", [128, 128], F32)
            iota_bf = sb("iota_bf", [128, 128], BF16)
            keysT = sb("keysT", [128, 16, 128], BF16)
            S_.dma(ident_b.t[:], ident_d, writes=[ident_b], q="pool")
            S_.dma(ident_f.t[:], ident_d, writes=[ident_f])
            S_.dma(iota_bf.t[:], iota128_d, writes=[iota_bf], q="pool")
            PT_t = es2.enter_context(nc.psum_tensor("PT2", [128, 1024], BF16))
            PQ_t = es2.enter_context(nc.psum_tensor("PQ", [128, 2048], F32))
            PH_t = es2.enter_context(nc.psum_tensor("PH", [128, 1024], F32))
            PX_t = es2.enter_context(nc.psum_tensor("PX", [128, 512], F32))
            PT = Buf(PT_t, "PT2")
            PR = PT
            PR_t = PT_t.bitcast(F32)
            PQ = [Buf(PQ_t, "PQ%d" % i) for i in range(4)]
            PHb = [Buf(PH_t, "PHb%d" % i) for i in range(2)]
            PX = Buf(PX_t, "PX")

            with ExitStack() as esp:
                S_.es = esp
                keysf = sb("keysf", [128, 16, 128], BF16)
                S_.dma(keysf.t[:], keys_d.rearrange("g k d -> k g d"), writes=[keysf], q="pool")
                for half in range(2):
                    S_.pe_group([lambda e, c=c, half=half: e.transpose(PT_t[:, c * 128:(c + 1) * 128],
                                                                        keysf.t[:, half * 8 + c, :], ident_b.t[:])
                                 for c in range(8)], reads=[keysf, ident_b], writes=[PT])
                    S_.a(lambda e, half=half: e.copy(
                        keysT.t[:, half * 8:(half + 1) * 8, :].rearrange("p c t -> p (c t)"), PT_t[:, :]),
                        reads=[PT], writes=[keysT])
                S_.barrier()
            S_.es = es2

            g2 = sb("g2", [128, D], F32)
            b2 = sb("b2", [128, D], F32)
            iota16 = sb("iota16", [128, 16], F32)
            S_.dma(g2.t[:], rep["ln2_g"], writes=[g2])
            S_.dma(b2.t[:], rep["ln2_b"], writes=[b2])
            S_.dma(iota16.t[:], iota16_d, writes=[iota16])

            zres = [sb("zres%d" % i, [128, D], F32) for i in range(4)]
            hTt = [sb("hTt%d" % i, [128, 8, 256], BF16) for i in range(2)]
            qTr = [sb("qTr%d" % i, [128, 16, 128], BF16) for i in range(2)]
            sc = sb("sc", [128, 16, 128], F32)
            scw = [sb("scw%d" % i, [128, 128], F32) for i in range(2)]
            tops = sb("tops", [128, 16, 16], F32)
            topi = sb("topi", [128, 16, 16], U32)
            topif = sb("topif", [128, 16, 16], F32)
            cand = sb("cand", [128, 8, 256], F32)
            candw = [sb("candw%d" % i, [128, 256], F32) for i in range(2)]
            fins = sb("fins", [128, 8, 16], F32)
            finp = sb("finp", [128, 8, 16], U32)
            pa_u = sb("pa_u", [128, 8, 16], U32)
            pb_u = sb("pb_u", [128, 8, 16], U32)
            pa_f = sb("pa_f", [128, 8, 16], F32)
            pb_f = sb("pb_f", [128, 8, 16], F32)
            oh = sb("oh", [128, 8, 16, 16], BF16)
            IJG = sb("IJG", [128, 3, 128], F32)
            IJGT = [sb("IJGT%d" % i, [128, 3, 128], F32) for i in range(4)]
            esm = sb("esm", [128, 8, 16], F32)
            zs = sb("zs", [128, 8, 1], F32)
            NPQ = 6
            Pb = [sb("Pb%d" % i, [128, 128], BF16) for i in range(NPQ)]
            Qb = [sb("Qb%d" % i, [128, 128], BF16) for i in range(NPQ)]
            GTs = sb("GTs", [128, 128, 256], BF16)
            NA = 4
            Ab = [sb("Ab%d" % i, [128, 256], BF16) for i in range(NA)]
            NR = 3
            utr = [sb("utr%d" % i, [128, 2, D], BF16) for i in range(NR)]
            vr = [sb("vr%d" % i, [128, 2, D], BF16) for i in range(NR)]
            z2 = sb("z2", [128, D], F32)
            yo = [sb("yo%d" % i, [128, D], F32) for i in range(2)]
            stats = sb("stats2", [128, 4, 6], F32)
            mv = sb("mv2", [128, 4, 2], F32)
            rstd = sb("rstd2", [128, 4], F32)
            epsb = sb("epsb2", [128, 1], F32)
            S_.v(lambda e: e.memset(epsb.t[:], LN_EPS), writes=[epsb])

            def layer_norm2(src, dst, g, b):
                for c in range(2):
                    S_.v(lambda e, c=c: e.bn_stats(stats.t[:, c, :], src.t[:, c * 512:(c + 1) * 512]),
                         reads=[src], writes=[stats])
                S_.v(lambda e: e.bn_aggr(mv.t[:, 0, :], stats.t[:, 0:2, :].rearrange("p a b -> p (a b)")),
                     reads=[stats], writes=[mv])
                S_.a(lambda e: e.activation(rstd.t[:, 0:1], mv.t[:, 0, 1:2], AF.Ln, bias=epsb.t[:, 0:1]),
                     reads=[mv, epsb], writes=[rstd])
                S_.a(lambda e: e.activation(rstd.t[:, 0:1], rstd.t[:, 0:1], AF.Exp, scale=-0.5),
                     reads=[rstd], writes=[rstd])
                S_.v(lambda e: e.tensor_scalar(dst.t[:], src.t[:], mv.t[:, 0, 0:1], rstd.t[:, 0:1],
                                               ALU.subtract, ALU.mult), reads=[src, mv, rstd], writes=[dst])
                S_.v(lambda e: e.tensor_mul(dst.t[:], dst.t[:], g.t[:]), reads=[dst, g], writes=[dst])
                S_.v(lambda e: e.tensor_add(dst.t[:], dst.t[:], b.t[:]), reads=[dst, b], writes=[dst])

            print("pass2 sbuf remaining", nc.sbuf_bytes_remaining)

            def routing(blk):
                sub = blk % 2
                tau = blk // 2
                ijgt = IJGT[blk % 4]
                qT = qTr[blk % 2]
                S_.dma(qT.t[:].rearrange("p c t -> p (c t)"), qts_d[blk], reads=[QT_s], writes=[qT])
                S_.dma(zres[blk % 4].t[:], zres_d[blk * 128:(blk + 1) * 128, :], reads=[ZR_s], writes=[zres[blk % 4]])
                S_.dma(hTt[tau % 2].t[:, :, sub * 128:(sub + 1) * 128], h1ts_d[blk].rearrange("p (c t) -> p c t", c=8),
                       reads=[H1T_s], writes=[hTt[tau % 2]])
                yield
                for qc in range(4):
                    S_.pe_group([lambda e, g4=g4, qc=qc: e.matmul(
                        PX_t[:, g4 * 128:(g4 + 1) * 128], qT.t[:, qc * 4 + g4, :], keysT.t[:, qc * 4 + g4, :],
                        start=True, stop=True) for g4 in range(4)], reads=[qT, keysT], writes=[PX])
                    yield
                    S_.a(lambda e, qc=qc: e.copy(sc.t[:, qc * 4:(qc + 1) * 4, :].rearrange("p c t -> p (c t)"),
                                                 PX_t[:, 0:512]), reads=[PX], writes=[sc])
                    yield
                for g in range(16):
                    sw = scw[g % 2]
                    S_.v(lambda e, g=g: e.max(tops.t[:, g, 0:8], sc.t[:, g, :]), reads=[sc], writes=[tops])
                    S_.v(lambda e, g=g: e.max_index(topi.t[:, g, 0:8], tops.t[:, g, 0:8], sc.t[:, g, :]),
                         reads=[sc, tops], writes=[topi])
                    S_.v(lambda e, g=g, sw=sw: e.match_replace(sw.t[:], tops.t[:, g, 0:8], sc.t[:, g, :], -1e30),
                         reads=[sc, tops], writes=[sw])
                    S_.v(lambda e, g=g, sw=sw: e.max(tops.t[:, g, 8:16], sw.t[:]), reads=[sw], writes=[tops])
                    S_.v(lambda e, g=g, sw=sw: e.max_index(topi.t[:, g, 8:16], tops.t[:, g, 8:16], sw.t[:]),
                         reads=[sw, tops], writes=[topi])
                    yield
                S_.v(lambda e: e.tensor_copy(topif.t[:], topi.t[:]), reads=[topi], writes=[topif])
                tv = tops.t[:].rearrange("p (h c) k -> p h c k", c=2)
                iv = topif.t[:].rearrange("p (h c) k -> p h c k", c=2)
                S_.v(lambda e: e.tensor_add(cand.t[:].rearrange("p h (a b) -> p h a b", a=16),
                                            tv[:, :, 0, :].unsqueeze(3).to_broadcast([128, 8, 16, 16]),
                                            tv[:, :, 1, :].unsqueeze(2).to_broadcast([128, 8, 16, 16])),
                     reads=[tops], writes=[cand])
                yield
                for h in range(8):
                    cw = candw[h % 2]
                    S_.v(lambda e, h=h: e.max(fins.t[:, h, 0:8], cand.t[:, h, :]), reads=[cand], writes=[fins])
                    S_.v(lambda e, h=h: e.max_index(finp.t[:, h, 0:8], fins.t[:, h, 0:8], cand.t[:, h, :]),
                         reads=[cand, fins], writes=[finp])
                    S_.v(lambda e, h=h, cw=cw: e.match_replace(cw.t[:], fins.t[:, h, 0:8], cand.t[:, h, :], -1e30),
                         reads=[cand, fins], writes=[cw])
                    S_.v(lambda e, h=h, cw=cw: e.max(fins.t[:, h, 8:16], cw.t[:]), reads=[cw], writes=[fins])
                    S_.v(lambda e, h=h, cw=cw: e.max_index(finp.t[:, h, 8:16], fins.t[:, h, 8:16], cw.t[:]),
                         reads=[cw, fins], writes=[finp])
                    yield
                S_.v(lambda e: e.tensor_single_scalar(pa_u.t[:], finp.t[:], 4, ALU.logical_shift_right),
                     reads=[finp], writes=[pa_u])
                S_.v(lambda e: e.tensor_single_scalar(pb_u.t[:], finp.t[:], 15, ALU.bitwise_and),
                     reads=[finp], writes=[pb_u])
                S_.v(lambda e: e.tensor_copy(pa_f.t[:], pa_u.t[:]), reads=[pa_u], writes=[pa_f])
                S_.v(lambda e: e.tensor_copy(pb_f.t[:], pb_u.t[:]), reads=[pb_u], writes=[pb_f])
                yield
                io4 = iota16.t[:].unsqueeze(1).unsqueeze(1).to_broadcast([128, 8, 16, 16])
                for (pf, ci, di) in ((pa_f, 0, 0), (pb_f, 1, 1)):
                    S_.v(lambda e, pf=pf: e.tensor_tensor(oh.t[:], pf.t[:].unsqueeze(3).to_broadcast([128, 8, 16, 16]),
                                                          io4, ALU.is_equal), reads=[pf, iota16], writes=[oh])
                    yield
                    S_.v(lambda e, ci=ci: e.tensor_mul(oh.t[:], oh.t[:],
                                                       iv[:, :, ci, :].unsqueeze(2).to_broadcast([128, 8, 16, 16])),
                         reads=[oh, topif], writes=[oh])
                    yield
                    S_.v(lambda e, di=di: e.tensor_reduce(IJG.t[:, di, :].rearrange("p (h k) -> p h k", h=8),
                                                          oh.t[:], AX.X, ALU.add), reads=[oh], writes=[IJG])
                    yield
                S_.v(lambda e: e.tensor_sub(esm.t[:], fins.t[:], fins.t[:, :, 0:1].to_broadcast([128, 8, 16])),
                     reads=[fins], writes=[esm])
                yield
                S_.a(lambda e: e.activation(esm.t[:], esm.t[:], AF.Exp), reads=[esm], writes=[esm])
                yield
                S_.v(lambda e: e.tensor_reduce(zs.t[:, :, 0], esm.t[:], AX.X, ALU.add), reads=[esm], writes=[zs])
                S_.v(lambda e: e.reciprocal(zs.t[:], zs.t[:]), reads=[zs], writes=[zs])
                S_.v(lambda e: e.tensor_mul(IJG.t[:, 2, :].rearrange("p (h k) -> p h k", h=8), esm.t[:],
                                            zs.t[:].to_broadcast([128, 8, 16])),
                     reads=[esm, zs, IJG], writes=[IJG])
                yield
                S_.pe_group([lambda e, c=c: e.transpose(PR_t[:, c * 128:(c + 1) * 128], IJG.t[:, c, :], ident_f.t[:])
                             for c in range(3)], reads=[IJG, ident_f], writes=[PR])
                yield
                S_.a(lambda e: e.copy(ijgt.t[:].rearrange("p c t -> p (c t)"), PR_t[:, 0:384]),
                     reads=[PR], writes=[ijgt])
                yield

            g_ring = [(PX, PX_t[:, 0:512]), (PHb[0], PH_t[:, 0:512]), (PHb[1], PH_t[:, 512:1024])]

            def gphase(blk):
                ijgt = IJGT[blk % 4]
                sub = blk % 2
                for t4 in range(32):
                    pg, pg_ap = g_ring[t4 % 3]
                    for u in range(4):
                        tok = t4 * 4 + u
                        pb_ = Pb[tok % NPQ]
                        qb_ = Qb[tok % NPQ]
                        S_.v(lambda e, pb_=pb_, tok=tok: e.tensor_scalar(pb_.t[:], iota_bf.t[:], ijgt.t[:, 0, tok:tok + 1],
                                                                         None, ALU.is_equal),
                             reads=[iota_bf, ijgt], writes=[pb_])
                        S_.v(lambda e, qb_=qb_, tok=tok: e.tensor_scalar(qb_.t[:], iota_bf.t[:], ijgt.t[:, 1, tok:tok + 1],
                                                                         ijgt.t[:, 2, tok:tok + 1], ALU.is_equal, ALU.mult),
                             reads=[iota_bf, ijgt], writes=[qb_])
                        S_.pe(lambda e, pb_=pb_, qb_=qb_, u=u, pg_ap=pg_ap: e.matmul(
                            pg_ap.rearrange("p (i u) -> p i u", u=4)[:, :, u], qb_.t[:], pb_.t[:], start=True, stop=True),
                            reads=[pb_, qb_], writes=[pg])
                    S_.a(lambda e, t4=t4, pg_ap=pg_ap: e.copy(
                        GTs.t[:, :, sub * 128 + t4 * 4:sub * 128 + (t4 + 1) * 4],
                        pg_ap.rearrange("p (i u) -> p i u", u=4)),
                        reads=[pg], writes=[GTs])

            LA = 1
            hid_ring = [(PHb[0], PH_t[:, 0:256]), (PHb[1], PH_t[:, 512:768])]

            def load_ut(g):
                if g < 64:
                    S_.dma(utr[g % NR].t[:].rearrange("p c f -> p (c f)"), uts_d[g],
                           reads=[UTs], writes=[utr[g % NR]])

            def load_v(g):
                if g < 64:
                    S_.dma(vr[g % NR].t[:].rearrange("p c f -> p (c f)"), vs_d[g],
                           reads=[Vs], writes=[vr[g % NR]])

            def chain(*gens):
                for g_ in gens:
                    for _ in g_:
                        yield

            for _ in chain(routing(0), routing(1)):
                pass
            for tau in range(NTILE):
                hT_ = hTt[tau % 2]
                gphase(2 * tau)
                gphase(2 * tau + 1)
                gen = chain(routing(2 * tau + 2), routing(2 * tau + 3)) if tau + 1 < NTILE else None
                load_ut(0)
                load_v(0)
                load_ut(1)
                load_v(1)
                for it in range(128 + LA):
                    if it < 128:
                        i = it
                        ut_ = utr[(i // 2) % NR]
                        a_ = Ab[i % NA]
                        ph, ph_ap = hid_ring[i % 2]
                        S_.pe_group([lambda e, dc=dc, ut_=ut_, i=i, ph_ap=ph_ap: e.matmul(
                            ph_ap, ut_.t[:, i % 2, dc * 128:(dc + 1) * 128], hT_.t[:, dc, :],
                            start=(dc == 0), stop=(dc == 7)) for dc in range(8)], reads=[ut_, hT_], writes=[ph])
                        S_.a(lambda e, a_=a_, ph_ap=ph_ap: e.activation(a_.t[:], ph_ap, AF.Gelu),
                             reads=[ph], writes=[a_])
                        S_.v(lambda e, a_=a_, i=i: e.tensor_mul(a_.t[:], a_.t[:], GTs.t[:, i, :]),
                             reads=[a_, GTs], writes=[a_])
                    if it >= LA:
                        i = it - LA
                        v_ = vr[(i // 2) % NR]
                        a_ = Ab[i % NA]
                        S_.pe_group([lambda e, n=n, sb_=sb_, a_=a_, v_=v_, i=i: e.matmul(
                            PQ_t[:, sb_ * 1024 + n * 512:sb_ * 1024 + (n + 1) * 512], a_.t[:, sb_ * 128:(sb_ + 1) * 128],
                            v_.t[:, i % 2, n * 512:(n + 1) * 512],
                            start=(i == 0), stop=(i == 127)) for sb_ in range(2) for n in range(2)],
                            reads=[a_, v_], writes=PQ)
                    if it % 2 == 0:
                        load_ut(it // 2 + 2)
                    if it >= LA and (it - LA) % 2 == 0:
                        load_v((it - LA) // 2 + 2)
                    if gen is not None and it >= 1:
                        next(gen, None)
                if gen is not None:
                    for _ in gen:
                        pass
                for sb_ in range(2):
                    blk = 2 * tau + sb_
                    S_.v(lambda e, sb_=sb_, blk=blk: e.tensor_add(z2.t[:], zres[blk % 4].t[:],
                                                                  PQ_t[:, sb_ * 1024:(sb_ + 1) * 1024]),
                         reads=[zres[blk % 4], PQ[2 * sb_], PQ[2 * sb_ + 1]], writes=[z2])
                    ob = yo[blk % 2]
                    layer_norm2(z2, ob, g2, b2)
                    S_.dma(y_d[blk * 128:(blk + 1) * 128, :], ob.t[:], reads=[ob], out_final=True)
            S_.finish()
    return nc, S_


_PROG_CACHE = {}


def prep_inputs(inp, NB, S, ncores):
    consts, _ = host_consts(S)
    f = lambda a: np.ascontiguousarray(np.asarray(a, dtype=np.float32))
    x = f(inp["x"])
    p = f(inp["p"])[0]
    shared = {
        "w_in": f(inp["w_in"])[0], "w_ret_o": f(inp["w_ret_o"])[0], "w_fox_o": f(inp["w_fox_o"])[0],
        "w_out": f(inp["w_out"])[0], "w_peer_q": f(inp["w_peer_q"])[0],
        "peer_sub_keys": f(inp["peer_sub_keys"])[0].reshape(16, 128, 128),
        "peer_u": f(inp["peer_u"])[0], "peer_v": f(inp["peer_v"])[0],
        "w_ple_gate": f(inp["w_ple_gate"])[0], "w_ple": f(inp["w_ple"])[0],
        "b_forget": f(inp["b_forget"])[0].reshape(8, 1),
    }
    rows = {"ln_emb_g": inp["ln_emb_g"], "ln_emb_b": inp["ln_emb_b"], "ln1_g": np.asarray(inp["ln1_g"])[0],
            "ln1_b": np.asarray(inp["ln1_b"])[0], "ln2_g": np.asarray(inp["ln2_g"])[0],
            "ln2_b": np.asarray(inp["ln2_b"])[0], "b_ple_gate": np.asarray(inp["b_ple_gate"])[0]}
    for k, v in rows.items():
        shared[k] = np.ascontiguousarray(np.broadcast_to(f(v).reshape(1, D), (128, D)))
    bb = f(inp["b_branch_gate"])[0]
    shared["bbr"] = np.ascontiguousarray(bb.reshape(2, 8, 128).transpose(2, 0, 1))
    shared.update(consts)
    in_maps = []
    for c in range(ncores):
        m = dict(shared)
        m["x"] = np.ascontiguousarray(x[c * NB:(c + 1) * NB].reshape(NB * S, D))
        m["p"] = np.ascontiguousarray(p[c * NB:(c + 1) * NB].reshape(NB * S, 256))
        in_maps.append(m)
    return in_maps


def kernel(**inputs):
    x = np.asarray(inputs["x"])
    B, S, _ = x.shape
    ncores = 8
    NB = B // ncores
    key = (NB, S)
    if key not in _PROG_CACHE:
        _PROG_CACHE[key] = build_program(NB, S)[0]
    nc = _PROG_CACHE[key]
    in_maps = prep_inputs(inputs, NB, S, ncores)
    res = run_bass_kernel_spmd(nc, in_maps, core_ids=list(range(ncores)))
    out = np.concatenate([np.asarray(r["y"]).reshape(NB, S, D) for r in res.results], axis=0)
    return out.astype(np.float32)
```
